# Optimizing a Trainium2 kernel written in Bass

```python
import math
import jax, jax.numpy as jnp
from jax import lax
import numpy as np

D_MODEL = 4096
BATCH = 2
SEQ = 4096
DEPTH = 4

N_BRANCH = 3
BRANCH_WIDTH = D_MODEL // 4
DA_HEADS = 8
DA_SUB = BRANCH_WIDTH // (2 * DA_HEADS)
DA_HEAD = 2 * DA_SUB
DA_ROT = DA_SUB // 4
LRU_WIDTH = BRANCH_WIDTH
LRU_BLOCKS = 8
LRU_BW = LRU_WIDTH // LRU_BLOCKS
CONV_W = 4
LRU_C = 8.0
MLA_HEADS = 8
MLA_NOPE = 128
MLA_ROPE = 64
MLA_V = BRANCH_WIDTH // MLA_HEADS
MLA_QK = MLA_NOPE + MLA_ROPE
Q_LORA = 3 * D_MODEL // 16
KV_LORA = D_MODEL // 8
GATE_RANK = D_MODEL // 16
FFN_HIDDEN = -(-(8 * D_MODEL) // (3 * 256)) * 256
ROPE_THETA = 500000.0
Q_BLOCK = 128
EPS = 1e-6
IN_SIZES = (DA_HEADS * DA_HEAD, DA_HEADS * DA_HEAD, DA_HEADS * DA_HEAD,
            LRU_WIDTH, LRU_WIDTH, Q_LORA, KV_LORA, MLA_ROPE, GATE_RANK)
IN_COLS = 3 * DA_HEADS * DA_HEAD + 2 * LRU_WIDTH + Q_LORA + KV_LORA + MLA_ROPE + GATE_RANK

kernel_name = 'hybrid_diffattn_rglru_mla_encoder'


def _rms_norm(x, g, eps=EPS):
    xf = x.astype(jnp.float32)
    y = xf * lax.rsqrt(jnp.mean(xf * xf, axis=-1, keepdims=True) + eps)
    return (y * g.astype(jnp.float32)).astype(x.dtype)


def _rope_tables(positions, rot_dim):
    inv = ROPE_THETA ** (-jnp.arange(0, rot_dim, 2, dtype=jnp.float32) / rot_dim)
    ang = positions.astype(jnp.float32)[..., None] * inv
    return jnp.cos(ang), jnp.sin(ang)


def _apply_rope(x, cos, sin):
    half = cos.shape[-1]
    extra = x.ndim - 3
    shp = cos.shape[:2] + (1,) * extra + (half,)
    c = cos.reshape(shp).astype(x.dtype)
    s = sin.reshape(shp).astype(x.dtype)
    x1 = x[..., :half]
    x2 = x[..., half:2 * half]
    return jnp.concatenate([x1 * c - x2 * s, x2 * c + x1 * s, x[..., 2 * half:]], axis=-1)


def _over_query_blocks(fn, q):
    b, h, s = q.shape[:3]
    nb = s // Q_BLOCK
    qb = jnp.moveaxis(q.reshape((b, h, nb, Q_BLOCK) + q.shape[3:]), 2, 0)
    out = lax.map(fn, qb)
    out = jnp.moveaxis(out, 0, 2)
    return out.reshape(b, h, s, out.shape[-1])


def _split_last(h, sizes):
    outs = []
    start = 0
    for n in sizes:
        outs.append(h[..., start:start + n])
        start += n
    return outs


def _diff_attention(xq, xk, xv, cos, sin, q_g, k_g, lam_p, subln_g, lam_init):
    b, s = xq.shape[:2]
    q = xq.reshape(b, s, DA_HEADS, 2, DA_SUB)
    k = xk.reshape(b, s, DA_HEADS, 2, DA_SUB)
    v = xv.reshape(b, s, DA_HEADS, DA_HEAD).transpose(0, 2, 1, 3)
    q = _apply_rope(_rms_norm(q, q_g), cos, sin).transpose(0, 2, 1, 3, 4)
    k = _apply_rope(_rms_norm(k, k_g), cos, sin).transpose(0, 2, 1, 3, 4)
    lp = lam_p.astype(jnp.float32)
    lam = jnp.exp(jnp.sum(lp[0] * lp[1])) - jnp.exp(jnp.sum(lp[2] * lp[3])) + lam_init
    scale = DA_SUB ** -0.5

    def blk(qb):
        sc = jnp.einsum('bhqnd,bhknd->bhnqk', qb, k).astype(jnp.float32) * scale
        p = jax.nn.softmax(sc, axis=-1)
        pd = p[:, :, 0] - lam * p[:, :, 1]
        return jnp.einsum('bhqk,bhkd->bhqd', pd.astype(v.dtype), v)

    o = _over_query_blocks(blk, q)
    o = _rms_norm(o, subln_g) * (1.0 - lam_init)
    return o.transpose(0, 2, 1, 3).reshape(b, s, DA_HEADS * DA_HEAD)


def _linear_scan(a, bx, reverse):
    def comb(left, right):
        a1, b1 = left
        a2, b2 = right
        return a1 * a2, a2 * b1 + b2
    _, h = lax.associative_scan(comb, (a, bx), axis=1, reverse=reverse)
    return h


def _rglru_bidir(u, conv_w, conv_b, gate_w, gate_b, lam_L):
    b, s = u.shape[:2]
    pad_l = CONV_W // 2
    uc = lax.conv_general_dilated(
        u, conv_w[:, None, :].astype(u.dtype), window_strides=(1,),
        padding=[(pad_l, CONV_W - 1 - pad_l)],
        dimension_numbers=('NWC', 'WIO', 'NWC'),
        feature_group_count=LRU_WIDTH) + conv_b
    ub = uc.reshape(b, s, LRU_BLOCKS, LRU_BW)
    g = jnp.einsum('bsnc,zgncd->zgbsnd', ub, gate_w).reshape(2, 2, b, s, LRU_WIDTH)
    g = jax.nn.sigmoid((g + gate_b[:, :, None, None, :]).astype(jnp.float32))
    r, i = g[:, 0], g[:, 1]
    log_a = -LRU_C * r * jax.nn.softplus(-lam_L.astype(jnp.float32))[:, None, None, :]
    a = jnp.exp(log_a)
    mult = jnp.sqrt(-jnp.expm1(2.0 * log_a))
    bx = mult * i * uc.astype(jnp.float32)[None]
    h_f = _linear_scan(a[0], bx[0], reverse=False)
    h_b = _linear_scan(a[1], bx[1], reverse=True)
    return (h_f + h_b).astype(u.dtype)


def _mla(c_q, c_kv, k_pe, cos, sin, q_a_g, w_uq, kv_a_g, w_ukv, q_g, k_g):
    b, s = c_q.shape[:2]
    q = (_rms_norm(c_q, q_a_g) @ w_uq).reshape(b, s, MLA_HEADS, MLA_QK)
    kv = (_rms_norm(c_kv, kv_a_g) @ w_ukv).reshape(b, s, MLA_HEADS, MLA_NOPE + MLA_V)
    k_nope, v = kv[..., :MLA_NOPE], kv[..., MLA_NOPE:]
    k_pe_h = jnp.broadcast_to(k_pe[:, :, None, :], (b, s, MLA_HEADS, MLA_ROPE))
    k = jnp.concatenate([k_pe_h, k_nope], axis=-1)
    q = _apply_rope(_rms_norm(q, q_g), cos, sin).transpose(0, 2, 1, 3)
    k = _apply_rope(_rms_norm(k, k_g), cos, sin).transpose(0, 2, 1, 3)
    v = v.transpose(0, 2, 1, 3)
    scale = MLA_QK ** -0.5

    def blk(qb):
        sc = jnp.einsum('bhqd,bhkd->bhqk', qb, k).astype(jnp.float32) * scale
        p = jax.nn.softmax(sc, axis=-1)
        return jnp.einsum('bhqk,bhkd->bhqd', p.astype(v.dtype), v)

    o = _over_query_blocks(blk, q)
    return o.transpose(0, 2, 1, 3).reshape(b, s, MLA_HEADS * MLA_V)


def setup_inputs(seed: int = 0) -> dict:
    key = jax.random.key(seed)
    ks = jax.random.split(key, 32)

    def nrm(k, shape, fan_in):
        return jax.random.normal(k, shape, jnp.float32) * (fan_in ** -0.5)

    def gain(k, shape):
        return 1.0 + 0.02 * jax.random.normal(k, shape, jnp.float32)

    def bias(k, shape, sc=0.02):
        return sc * jax.random.normal(k, shape, jnp.float32)

    L = DEPTH
    x = jax.random.normal(ks[0], (BATCH, SEQ, D_MODEL), jnp.float32)
    positions = (jnp.arange(SEQ, dtype=jnp.int32)[None, :]
                 + jax.random.randint(ks[1], (BATCH, 1), 0, 1024, dtype=jnp.int32))
    a8 = jax.random.uniform(ks[12], (L, 2, LRU_WIDTH), jnp.float32, 0.9, 0.999)
    a_base = a8 ** (1.0 / LRU_C)
    lru_L = jnp.log(a_base) - jnp.log1p(-a_base)
    return {
        'x': x,
        'positions': positions,
        'mixer_norm_g': gain(ks[2], (L, D_MODEL)),
        'w_in': nrm(ks[3], (L, D_MODEL, IN_COLS), D_MODEL),
        'da_q_norm_g': gain(ks[4], (L, DA_SUB)),
        'da_k_norm_g': gain(ks[5], (L, DA_SUB)),
        'da_lambda': 0.1 * jax.random.normal(ks[6], (L, 4, DA_SUB), jnp.float32),
        'da_subln_g': gain(ks[7], (L, DA_HEAD)),
        'lru_conv_w': nrm(ks[8], (L, CONV_W, LRU_WIDTH), CONV_W),
        'lru_conv_b': bias(ks[9], (L, LRU_WIDTH)),
        'lru_gate_w': nrm(ks[10], (L, 2, 2, LRU_BLOCKS, LRU_BW, LRU_BW), LRU_BW),
        'lru_gate_b': bias(ks[11], (L, 2, 2, LRU_WIDTH)),
        'lru_L': lru_L,
        'mla_q_a_norm_g': gain(ks[13], (L, Q_LORA)),
        'mla_w_uq': nrm(ks[14], (L, Q_LORA, MLA_HEADS * MLA_QK), Q_LORA),
        'mla_kv_a_norm_g': gain(ks[15], (L, KV_LORA)),
        'mla_w_ukv': nrm(ks[16], (L, KV_LORA, MLA_HEADS * (MLA_NOPE + MLA_V)), KV_LORA),
        'mla_q_norm_g': gain(ks[17], (L, MLA_QK)),
        'mla_k_norm_g': gain(ks[18], (L, MLA_QK)),
        'w_gate_up': nrm(ks[19], (L, GATE_RANK, N_BRANCH * D_MODEL), GATE_RANK),
        'b_gate': bias(ks[20], (L, N_BRANCH, D_MODEL)),
        'w_branch_out': nrm(ks[21], (L, N_BRANCH, BRANCH_WIDTH, D_MODEL), BRANCH_WIDTH),
        'w_out': nrm(ks[22], (L, D_MODEL, D_MODEL), D_MODEL),
        'ffn_norm_g': gain(ks[23], (L, D_MODEL)),
        'w_ffn_gate': nrm(ks[24], (L, D_MODEL, FFN_HIDDEN), D_MODEL),
        'w_ffn_up': nrm(ks[25], (L, D_MODEL, FFN_HIDDEN), D_MODEL),
        'w_ffn_down': nrm(ks[26], (L, FFN_HIDDEN, D_MODEL), FFN_HIDDEN),
    }


def reference(x, positions, mixer_norm_g, w_in, da_q_norm_g, da_k_norm_g, da_lambda, da_subln_g,
              lru_conv_w, lru_conv_b, lru_gate_w, lru_gate_b, lru_L,
              mla_q_a_norm_g, mla_w_uq, mla_kv_a_norm_g, mla_w_ukv, mla_q_norm_g, mla_k_norm_g,
              w_gate_up, b_gate, w_branch_out, w_out,
              ffn_norm_g, w_ffn_gate, w_ffn_up, w_ffn_down):
    b, s = x.shape[:2]
    cos_a, sin_a = _rope_tables(positions, DA_ROT)
    cos_c, sin_c = _rope_tables(positions, MLA_ROPE)
    for l in range(DEPTH):
        lam_init = 0.8 - 0.6 * math.exp(-0.3 * l)
        xn = _rms_norm(x, mixer_norm_g[l])
        h = xn @ w_in[l]
        a_q, a_k, a_v, b_gate_in, b_x, c_qa, c_kva, c_kpe, g_lr = _split_last(h, IN_SIZES)
        o_a = _diff_attention(a_q, a_k, a_v, cos_a, sin_a, da_q_norm_g[l], da_k_norm_g[l],
                              da_lambda[l], da_subln_g[l], lam_init)
        o_b = _rglru_bidir(b_x, lru_conv_w[l], lru_conv_b[l], lru_gate_w[l], lru_gate_b[l],
                           lru_L[l]) * jax.nn.gelu(b_gate_in)
        o_c = _mla(c_qa, c_kva, c_kpe, cos_c, sin_c, mla_q_a_norm_g[l], mla_w_uq[l],
                   mla_kv_a_norm_g[l], mla_w_ukv[l], mla_q_norm_g[l], mla_k_norm_g[l])
        gates = jax.nn.sigmoid((g_lr @ w_gate_up[l]).reshape(b, s, N_BRANCH, D_MODEL) + b_gate[l])
        branches = jnp.stack([o_a, o_b, o_c], axis=0)
        y_br = jnp.einsum('nbsc,ncd->bsnd', branches, w_branch_out[l])
        x = x + jnp.sum(gates * y_br, axis=2) @ w_out[l]
        xn2 = _rms_norm(x, ffn_norm_g[l])
        x = x + (jax.nn.silu(xn2 @ w_ffn_gate[l]) * (xn2 @ w_ffn_up[l])) @ w_ffn_down[l]
    return x
```

```python
import math
from contextlib import ExitStack
import numpy as np
import concourse.bass as bass
import concourse.mybir as mybir
from concourse.bass_utils import run_bass_kernel_spmd

F32, BF16, I32 = mybir.dt.float32, mybir.dt.bfloat16, mybir.dt.int32
AF = mybir.ActivationFunctionType
ALU = mybir.AluOpType
AX = mybir.AxisListType

D = 4096; T = 1024; S = 4096; NL = 4; FH = 11008; INC = 6720
EPS = 1e-6
GROUPS = [[0, 1, 2, 3], [4, 5, 6, 7]]
V_G1 = 0; V_G2 = 32; V_DAQ = 64; V_DAK = 65; V_DASUB = 66; V_LAMC = 67; V_LAM = 69; V_CONVW = 325; V_CONVB = 333
V_GATEB = 335; V_LRUL = 343; V_MQA = 347; V_MKVA = 353; V_MQGR = 357; V_MQGN = 358; V_MKGR = 359; V_MKGN = 360
V_BGATE = 361; V_QM = 457; NV = 461
C_GSUM = 0; C_SWDA = 128; C_SWMLA = 256; C_INVDA = 320; C_SGNDA = 321; C_INVMLA = 322; C_SGNMLA = 323; NCST = 324
TWO_PI = 2.0 * math.pi
MAGIC = 12582912.0
WNAMES = [("w_in", D, INC), ("w_uq", 768, 1536), ("w_ukv", 512, 2048), ("w_gu", 256, 3 * D), ("w_bo", 3072, D),
          ("w_out", D, D), ("w_fg", D, FH), ("w_fu", D, FH), ("w_fd", FH, D)]


class Rg:
    __slots__ = ("w", "r")

    def __init__(self):
        self.w = {}
        self.r = {}


class Tile:
    def __init__(self, t, rg=None, name=""):
        self.t = t
        self.rg = rg or Rg()
        self.name = name


class Ring:
    def __init__(self, tiles):
        self.tiles = tiles
        self.i = 0

    def next(self):
        t = self.tiles[self.i % len(self.tiles)]
        self.i += 1
        return t


class K:
    def __init__(self, nc, es):
        self.nc = nc
        self.es = es
        self.eng = dict(pe=nc.tensor, act=nc.scalar, dve=nc.vector, pool=nc.gpsimd, sp=nc.sync)
        self.sem = {}
        self.cnt = {}
        self.known = {e: {} for e in self.eng}
        for e in self.eng:
            self.newsem(e)
        self.uid = 0

    def newsem(self, key):
        if key not in self.sem:
            self.sem[key] = self.es.enter_context(self.nc.semaphore("s_" + key))
            self.cnt[key] = 0
        return key

    def _waits(self, e, reads, writes, part=False):
        need = {}
        for d in reads:
            for k, v in d.w.items():
                if need.get(k, 0) < v:
                    need[k] = v
        for d in writes:
            if not part:
                for k, v in d.w.items():
                    if need.get(k, 0) < v:
                        need[k] = v
            for k, v in d.r.items():
                if need.get(k, 0) < v:
                    need[k] = v
        kn = self.known[e]
        for k, v in need.items():
            if e == "pe" and k == "pe":
                continue
            if kn.get(k, 0) >= v:
                continue
            self.eng[e].wait_ge(self.sem[k], v)
            kn[k] = v

    def op(self, e, fn, reads=(), writes=(), acc=()):
        self._waits(e, reads, writes)
        fn(self.eng[e]).then_inc(self.sem[e], 1)
        self.cnt[e] += 1
        c = self.cnt[e]
        for d in reads:
            d.r[e] = c
        for d in writes:
            d.w = {e: c}
            d.r = {}
        for d in acc:
            d.w = {e: c}

    def dma(self, q, out, in_, reads, writes, sem, part=False):
        self.newsem(sem)
        self._waits(q, reads, writes, part=part)
        self.eng[q].dma_start(out=out, in_=in_).then_inc(self.sem[sem], 16)
        self.cnt[sem] += 16
        c = self.cnt[sem]
        for d in reads:
            d.r[sem] = c
        for d in writes:
            if part:
                d.w[sem] = c
            else:
                d.w = {sem: c}
            d.r = {}

    def cc(self, in_h, out_h, in_rg, out_rg, sem):
        self.newsem(sem)
        self._waits("pool", [in_rg], [out_rg], part=True)
        self.nc.gpsimd.collective_compute("AllGather", ALU.bypass, replica_groups=GROUPS,
                                          ins=[in_h.ap().opt()], outs=[out_h.ap().opt()]).then_inc(self.sem[sem])
        self.cnt[sem] += 1
        c = self.cnt[sem]
        in_rg.r[sem] = c
        out_rg.w[sem] = c
        out_rg.r = {}

    def barrier(self):
        for e in self.eng:
            kn = self.known[e]
            for k, v in self.cnt.items():
                if v > 0 and kn.get(k, 0) < v:
                    self.eng[e].wait_ge(self.sem[k], v)
                    kn[k] = v

    def sb(self, st, name, shape, dt):
        self.uid += 1
        return Tile(st.enter_context(self.nc.sbuf_tensor(f"{name}_{self.uid}", shape, dt)), name=name)

    def ring(self, st, name, n, shape, dt):
        return Ring([self.sb(st, f"{name}{i}", shape, dt) for i in range(n)])

    def act(self, out, in_, func, reads, writes, **kw):
        self.op("act", lambda e: e.activation(out=out, in_=in_, func=func, **kw), reads, writes)

    def tt(self, eng, out, a, b, op, reads, writes):
        self.op(eng, lambda e: e.tensor_tensor(out=out, in0=a, in1=b, op=op), reads, writes)

    def ts(self, eng, out, a, s1, s2, op0, op1, reads, writes):
        if s2 is None:
            self.op(eng, lambda e: e.tensor_scalar(out=out, in0=a, scalar1=s1, scalar2=None, op0=op0), reads, writes)
        else:
            self.op(eng, lambda e: e.tensor_scalar(out=out, in0=a, scalar1=s1, scalar2=s2, op0=op0, op1=op1), reads, writes)

    def stt(self, eng, out, in0, scalar, in1, op0, op1, reads, writes):
        self.op(eng, lambda e: e.scalar_tensor_tensor(out=out, in0=in0, scalar=scalar, in1=in1, op0=op0, op1=op1),
                reads, writes)


def build(depth):
    nc = bass.Bass("TRN2", target_bir_lowering=False)
    es = ExitStack()
    k = K(nc, es)
    ein = lambda name, shape, dt: nc.dram_tensor(name, shape, dt, kind="ExternalInput")
    xT_in = ein("xT_in", [D, T], F32)
    pos_in = ein("pos_in", [128, T], I32)
    vec_in = ein("vec_in", [128, depth * NV], F32)
    cst_in = ein("cst_in", [128, NCST], F32)
    lgw_in = ein("lgw_in", [depth * 8 * 128, 128], F32)
    wsh = {}
    wfull = {}
    wrg = {}
    for (nm, kk, nn) in WNAMES:
        wsh[nm] = ein(nm + "_w", [depth * kk, nn], F32)
    out_d = nc.dram_tensor("out", [D, T], F32, kind="ExternalOutput")
    R_x = Rg(); R_out = Rg()
    itn = lambda name, shape, dt=F32: (nc.dram_tensor(name, shape, dt), Rg())
    R0 = Rg()
    for l in range(depth):
        for (nm, kk, nn) in WNAMES:
            wfull[(nm, l)] = wsh[nm].ap()[l * kk:(l + 1) * kk, :]
            wrg[(nm, l)] = R0
    qda_d, R_qda = itn("qda_d", [1024, T])
    R_kpl = [Rg(), Rg()]; R_kpa = [Rg(), Rg()]; R_vpl = [Rg(), Rg()]; R_vpa = [Rg(), Rg()]
    R_ul = Rg(); R_ua = Rg(); R_obl = Rg(); R_oba = Rg()
    kpl = [nc.dram_tensor(f"kpl{i}", [256, T], F32) for i in range(10)]
    kpa = [nc.dram_tensor(f"kpa{i}", [1024, T], F32) for i in range(10)]
    vl = [[nc.dram_tensor(f"vl{a}_{i}", [256, 1024], F32) for i in range(4)] for a in range(2)]
    va = [[nc.dram_tensor(f"va{a}_{i}", [1024, 1024], F32) for i in range(4)] for a in range(2)]
    ul = [nc.dram_tensor(f"ul{i}", [256, T], F32) for i in range(4)]
    ua = [nc.dram_tensor(f"ua{i}", [1024, T], F32) for i in range(4)]
    obl = [nc.dram_tensor(f"obl{i}", [64, S], F32) for i in range(4)]
    oba = [nc.dram_tensor(f"oba{i}", [256, S], F32) for i in range(4)]
    kl_ap = lambda f0, rows, cs: kpl[f0 // 256].ap()[f0 % 256:f0 % 256 + rows, cs]
    ka_ap = lambda r, f0, rows: kpa[f0 // 256].ap()[r * 256 + f0 % 256:r * 256 + f0 % 256 + rows, :]
    vl_ap = lambda a, tok0, c0: vl[a][tok0 // 256].ap()[tok0 % 256:tok0 % 256 + 128, c0:c0 + 128]
    gel_d, R_gel = itn("gel_d", [1024, T])
    cqa_d, R_cqa = itn("cqa_d", [768, T])
    ckv_d, R_ckv = itn("ckv_d", [512, T])
    kpe_d, R_kpe = itn("kpe_d", [64, T])
    glr_d, R_glr = itn("glr_d", [256, T])
    mqr_d, R_mqr = itn("mqr_d", [512, T])
    mqn_d, R_mqn = itn("mqn_d", [1024, T])
    oa_d, R_oa = itn("oa_d", [1024, T])
    oc_d, R_oc = itn("oc_d", [1024, T])
    hT_d, R_hT = itn("hT_d", [FH, T])

    g = ExitStack()
    es.enter_context(g)
    ps = []
    psrg = []
    for i in range(8):
        ps.append(es.enter_context(nc.psum_tensor(f"psb{i}", [128, 512], F32)))
        psrg.append(Rg())
    vec = k.sb(g, "vec", [128, depth * NV], F32)
    cstf = k.sb(g, "cstf", [128, NCST], F32)
    cstb = k.sb(g, "cstb", [128, 320], BF16)
    ones = k.sb(g, "ones", [128, 128], BF16)
    Cda = k.sb(g, "Cda", [128, T], F32); Sda = k.sb(g, "Sda", [128, T], F32)
    Cml = k.sb(g, "Cml", [128, T], F32); Sml = k.sb(g, "Sml", [128, T], F32)
    k.dma("sp", vec.t[:, :], vec_in.ap(), [R0], [vec.rg], "ld_vec")
    k.dma("sp", cstf.t[:, :], cst_in.ap(), [R0], [cstf.rg], "ld_cst")
    k.op("dve", lambda e: e.tensor_copy(out=cstb.t[:, :], in_=cstf.t[:, 0:320]), [cstf.rg], [cstb.rg])
    k.op("dve", lambda e: e.memset(ones.t[:, :], 1.0), [], [ones.rg])
    gsum = cstb.t[:, C_GSUM:C_GSUM + 128]
    swda = cstb.t[:, C_SWDA:C_SWDA + 128]
    swml = cstb.t[0:64, C_SWMLA:C_SWMLA + 64]

    def vcol(l, c, rows=128):
        return vec.t[0:rows, l * NV + c:l * NV + c + 1]

    def evict_copy(out_ap, out_rgs, b, rows=128, eng="act"):
        if eng == "act":
            k.act(out_ap, ps[b][0:rows, :], AF.Copy, [psrg[b]], out_rgs)
        else:
            k.op("dve", lambda e: e.tensor_copy(out=out_ap, in_=ps[b][0:rows, :]), [psrg[b]], out_rgs)

    def rstd_from(dst, src_ap, src_rgs, inv_n, rows=128):
        d = dst.t[0:rows, :] if not isinstance(dst, tuple) else dst[0]
        rg = dst.rg if not isinstance(dst, tuple) else dst[1]
        k.ts("dve", d, src_ap, inv_n, EPS, ALU.mult, ALU.add, src_rgs, [rg])
        k.act(d, d, AF.Sqrt, [rg], [rg])
        k.op("dve", lambda e: e.reciprocal(out=d, in_=d), [rg], [rg])

    with ExitStack() as ph:
        posi = k.sb(ph, "posi", [128, T], I32)
        posf = k.sb(ph, "posf", [128, T], F32)
        ang = k.sb(ph, "ang", [128, T], F32)
        tq = k.sb(ph, "tq", [128, T], F32)
        k.dma("sp", posi.t[:, :], pos_in.ap(), [R0], [posi.rg], "ld_pos")
        k.op("dve", lambda e: e.tensor_copy(out=posf.t[:, :], in_=posi.t[:, :]), [posi.rg], [posf.rg])
        for (rows, cinv, csgn, Ct, St) in ((128, C_INVDA, C_SGNDA, Cda, Sda), (64, C_INVMLA, C_SGNMLA, Cml, Sml)):
            for which, dst in ((0, St), (1, Ct)):
                a_ = ang.t[0:rows, :]; t_ = tq.t[0:rows, :]; d_ = dst.t[0:rows, :]
                k.ts("dve", a_, posf.t[0:rows, :], cstf.t[0:rows, cinv:cinv + 1], None, ALU.mult, None,
                     [posf.rg, cstf.rg], [ang.rg])
                if which == 1:
                    k.ts("dve", a_, a_, math.pi / 2, None, ALU.add, None, [ang.rg], [ang.rg])
                k.ts("dve", t_, a_, 1.0 / TWO_PI, None, ALU.mult, None, [ang.rg], [tq.rg])
                k.ts("dve", t_, t_, MAGIC, None, ALU.add, None, [tq.rg], [tq.rg])
                k.ts("dve", t_, t_, -MAGIC, None, ALU.add, None, [tq.rg], [tq.rg])
                k.stt("dve", t_, t_, -TWO_PI, a_, ALU.mult, ALU.add, [tq.rg, ang.rg], [tq.rg])
                k.ts("dve", t_, t_, -3.1415925, 3.1415925, ALU.max, ALU.min, [tq.rg], [tq.rg])
                k.act(d_, t_, AF.Sin, [tq.rg], [dst.rg])
                if which == 0:
                    k.ts("dve", d_, d_, cstf.t[0:rows, csgn:csgn + 1], None, ALU.mult, None, [dst.rg, cstf.rg], [dst.rg])
        k.barrier()

    def gemm(st, name, entries, ngroups, evac, banks, NS=3, ntt=2):
        e0 = entries(0)
        rings = [k.sb(st, f"{name}_r{j}", [128, NS, e["kc"], e["w"] if "wmax" not in e else e["wmax"]], BF16)
                 for j, e in enumerate(e0)]
        slot_rg = [Rg() for _ in range(NS)]

        def load(gi):
            s = gi % NS
            for j, e in enumerate(entries(gi)):
                Wv = e["W"].rearrange("(k p) n -> p k n", p=128)
                for k0 in range(0, e["kc"], 16):
                    k1 = min(e["kc"], k0 + 16)
                    kr0 = e.get("krow0", 0)
                    k.dma("pool", rings[j].t[:, s, k0:k1, 0:e["w"]], Wv[:, kr0 + k0:kr0 + k1, e["c0"]:e["c0"] + e["w"]],
                          [e["Wrg"]], [slot_rg[s]], f"{name}_w{s}", part=True)
        PF = NS - 1
        for gi in range(min(PF, ngroups)):
            load(gi)
        bi = 0
        for gi in range(ngroups):
            if gi + PF < ngroups:
                load(gi + PF)
            s = gi % NS
            ents = entries(gi)
            for tt in range(ntt):
                bl = []
                for j, e in enumerate(ents):
                    b = banks[bi % len(banks)]; bi += 1
                    bl.append(b)
                    w, kc = e["w"], e["kc"]
                    if not e.get("tm"):
                        for kk_ in range(kc):
                            rap, rrg = e["rhs"](kk_, tt)
                            first = kk_ == 0
                            k.op("pe", lambda en, b=b, w=w, kk_=kk_, rap=rap, j=j, kc=kc: en.matmul(
                                ps[b][0:w, :], rings[j].t[:, s, kk_, 0:w], rap, start=(kk_ == 0), stop=(kk_ == kc - 1)),
                                [slot_rg[s], rrg], [psrg[b]] if first else [], [] if first else [psrg[b]])
                    else:
                        for tb in range(4):
                            for kk_ in range(kc):
                                lap, lrg = e["lhs"](kk_, tt * 4 + tb)
                                first = (tb == 0 and kk_ == 0)
                                k.op("pe", lambda en, b=b, w=w, kk_=kk_, lap=lap, j=j, kc=kc, tb=tb: en.matmul(
                                    ps[b][:, tb * 128:tb * 128 + w], lap, rings[j].t[:, s, kk_, 0:w],
                                    start=(kk_ == 0), stop=(kk_ == kc - 1)),
                                    [slot_rg[s], lrg], [psrg[b]] if first else [], [] if first else [psrg[b]])
                evac(gi, tt, bl)

    def rmsnorm(st, name, kc, src_chunk, gcol, xn, banks):
        n = kc * 128
        rstd = k.sb(st, name + "_rstd", [128, T], F32)
        sq = k.ring(st, name + "_sq", 2, [128, T], BF16)
        b0, b1 = banks
        for c in range(kc):
            xa, xrg = src_chunk(c, 0)
            s_ = sq.next()
            k.act(s_.t[:, :], xa, AF.Square, [xrg], [s_.rg])
            for tt, b in ((0, b0), (1, b1)):
                first = c == 0
                k.op("pe", lambda en, b=b, tt=tt, s_=s_, c=c: en.matmul(ps[b][:, :], ones.t[:, :], s_.t[:, tt * 512:(tt + 1) * 512],
                                                                    start=(c == 0), stop=(c == kc - 1)),
                     [ones.rg, s_.rg], [psrg[b]] if first else [], [] if first else [psrg[b]])
        for tt, b in ((0, b0), (1, b1)):
            rstd_from((rstd.t[:, tt * 512:(tt + 1) * 512], rstd.rg), ps[b][:, :], [psrg[b]], 1.0 / n)
        for c in range(kc):
            xa, xrg = src_chunk(c, 1)
            k.stt("dve", xn.t[:, c, :], xa, gcol(c), rstd.t[:, :], ALU.mult, ALU.mult, [xrg, rstd.rg, vec.rg], [xn.rg])

    def dram_chunk_loader(st, name, src_h, src_rg):
        ring = k.ring(st, name + "_xr", 3, [128, T], F32)

        def f(c, _pass):
            t_ = ring.next()
            k.dma("sp", t_.t[:, :], src_h.ap()[c * 128:(c + 1) * 128, :], [src_rg], [t_.rg], f"{name}_x{(ring.i - 1) % 3}")
            return t_.t[:, :], t_.rg
        return f

    def store(ob, dst_ap, dst_rg, rows=128, cols=512, ap=None):
        src = ap if ap is not None else ob.t[0:rows, 0:cols]
        k.dma("sp", dst_ap, src, [ob.rg], [dst_rg], "st_" + ob.name, part=True)

    def norm_rope(wk, obr, b, tt, rows, gq, inv_n, rstd_t, swap_ap, Ct, St, aux_b, src_ap=None, src_rgs=None):
        src_ap = src_ap if src_ap is not None else ps[b][0:rows, :]
        src_rgs = src_rgs if src_rgs is not None else [psrg[b]]
        Y = wk.next(); Yb = wk.next(); T1 = wk.next(); ob = obr.next()
        k.stt("dve", Y.t[0:rows, :], src_ap, gq, rstd_t.t[0:rows, :], ALU.mult, ALU.mult, src_rgs + [rstd_t.rg, vec.rg], [Y.rg])
        k.act(Ybf_of(Yb)[0:rows, :], Y.t[0:rows, :], AF.Copy, [Y.rg], [Yb.rg])
        k.op("pe", lambda en: en.matmul(ps[aux_b][0:rows, :], swap_ap, Ybf_of(Yb)[0:rows, :], start=True, stop=True),
             [Yb.rg, cstb.rg], [psrg[aux_b]])
        cs = slice(tt * 512, (tt + 1) * 512)
        k.tt("dve", T1.t[0:rows, :], Y.t[0:rows, :], Ct.t[0:rows, cs], ALU.mult, [Y.rg, Ct.rg], [T1.rg])
        k.tt("dve", Y.t[0:rows, :], ps[aux_b][0:rows, :], St.t[0:rows, cs], ALU.mult, [psrg[aux_b], St.rg], [Y.rg])
        k.tt("pool", ob.t[0:rows, :], T1.t[0:rows, :], Y.t[0:rows, :], ALU.add, [T1.rg, Y.rg], [ob.rg])
        return ob

    bfpool = {}

    def Ybf_of(tile):
        return bfpool[id(tile)].t

    def mk_wk(st, name, n):
        r = k.ring(st, name, n, [128, 512], F32)
        for t_ in r.tiles:
            bfpool[id(t_)] = k.sb(st, name + "bf", [128, 512], BF16)
        return r

    for l in range(depth):
        xsrc_h, xsrc_rg = (xT_in, R0) if l == 0 else (out_d, R_out)
        W = lambda nm: wfull[(nm, l)]
        WR = lambda nm: wrg[(nm, l)]
        with ExitStack() as ph:
            xn = k.sb(ph, "xn", [128, 32, T], BF16)
            rmsnorm(ph, "n1", 32, dram_chunk_loader(ph, "n1", xsrc_h, xsrc_rg), lambda c: vcol(l, V_G1 + c), xn, (6, 7))
            wk = mk_wk(ph, "wk", 6)
            obr = k.ring(ph, "ob", 4, [128, 512], F32)
            rst = k.ring(ph, "rst", 2, [128, 512], F32)
            sqr = k.ring(ph, "sqb", 2, [128, 512], BF16)
            chunks = [(c * 128, 128) for c in range(50)] + [(6400, 64), (6464, 128), (6592, 128)]

            def ent(gi):
                c0, w = chunks[gi]
                tm = 16 <= gi < 24
                return [dict(W=W("w_in"), Wrg=WR("w_in"), c0=c0, w=w, wmax=128, kc=32, tm=tm,
                             rhs=lambda kk_, tt: (xn.t[:, kk_, tt * 512:(tt + 1) * 512], xn.rg),
                             lhs=lambda kk_, tb: (xn.t[:, kk_, tb * 128:(tb + 1) * 128], xn.rg))]

            def evac(gi, tt, bl):
                b = bl[0]
                cs = slice(tt * 512, (tt + 1) * 512)
                if gi < 16:
                    s_ = sqr.next()
                    k.act(s_.t[:, :], ps[b][:, :], AF.Square, [psrg[b]], [s_.rg])
                    k.op("pe", lambda en: en.matmul(ps[6][:, :], gsum, s_.t[:, :], start=True, stop=True), [s_.rg, cstb.rg], [psrg[6]])
                    r_ = rst.next()
                    rstd_from(r_, ps[6][:, :], [psrg[6]], 1.0 / 64)
                    ob = norm_rope(wk, obr, b, tt, 128, vcol(l, V_DAQ if gi < 8 else V_DAK), 1.0 / 64, r_, swda, Cda, Sda, 7)
                    if gi < 8:
                        store(ob, qda_d.ap()[gi * 128:(gi + 1) * 128, cs], R_qda)
                    else:
                        store(ob, kl_ap((gi - 8) * 128, 128, cs), R_kpl[0])
                        if gi == 15 and tt == 1:
                            for i in range(4):
                                k.cc(kpl[i], kpa[i], R_kpl[0], R_kpa[0], "cc_kda")
                elif gi < 24:
                    ob = obr.next()
                    evict_copy(ob.t[:, :], [ob.rg], b)
                    h = gi - 16
                    for tb in range(4):
                        store(ob, vl_ap(0, (tt * 4 + tb) * 128, h * 128), R_vpl[0], ap=ob.t[:, tb * 128:(tb + 1) * 128])
                    if gi == 23 and tt == 1:
                        for i in range(4):
                            k.cc(vl[0][i], va[0][i], R_vpl[0], R_vpa[0], "cc_vda")
                elif gi < 32:
                    X = wk.next(); T1 = wk.next(); ob = obr.next()
                    evict_copy(X.t[:, :], [X.rg], b)
                    k.tt("dve", T1.t[:, :], X.t[:, :], X.t[:, :], ALU.mult, [X.rg], [T1.rg])
                    k.ts("dve", T1.t[:, :], T1.t[:, :], 0.044715, 1.0, ALU.mult, ALU.add, [T1.rg], [T1.rg])
                    k.tt("dve", T1.t[:, :], T1.t[:, :], X.t[:, :], ALU.mult, [T1.rg, X.rg], [T1.rg])
                    k.act(T1.t[:, :], T1.t[:, :], AF.Sigmoid, [T1.rg], [T1.rg], scale=2.0 * math.sqrt(2.0 / math.pi))
                    k.tt("pool", ob.t[:, :], T1.t[:, :], X.t[:, :], ALU.mult, [T1.rg, X.rg], [ob.rg])
                    store(ob, gel_d.ap()[(gi - 24) * 128:(gi - 23) * 128, cs], R_gel)
                else:
                    ob = obr.next()
                    rows = chunks[gi][1]
                    evict_copy(ob.t[0:rows, :], [ob.rg], b, rows=rows)
                    if gi < 40:
                        f0 = (gi - 32) * 128
                        store(ob, ul[f0 // 256].ap()[f0 % 256:f0 % 256 + 128, cs], R_ul)
                        if gi == 39 and tt == 1:
                            for i in range(4):
                                k.cc(ul[i], ua[i], R_ul, R_ua, "cc_u")
                    elif gi < 46:
                        store(ob, cqa_d.ap()[(gi - 40) * 128:(gi - 39) * 128, cs], R_cqa)
                    elif gi < 50:
                        store(ob, ckv_d.ap()[(gi - 46) * 128:(gi - 45) * 128, cs], R_ckv)
                    elif gi == 50:
                        store(ob, kpe_d.ap()[0:64, cs], R_kpe, rows=64)
                    else:
                        store(ob, glr_d.ap()[(gi - 51) * 128:(gi - 50) * 128, cs], R_glr)
            gemm(ph, "g1", ent, len(chunks), evac, [0, 1, 2, 3, 4, 5])
            k.barrier()
        with ExitStack() as ph:
            cqa = k.sb(ph, "cqa", [128, 6, T], F32)
            ckv = k.sb(ph, "ckv", [128, 4, T], F32)
            kpe = k.sb(ph, "kpe", [64, T], F32)
            sqk = k.sb(ph, "sqk", [64, T], BF16)
            cqn = k.sb(ph, "cqn", [128, 6, T], BF16)
            ckn = k.sb(ph, "ckn", [128, 4, T], BF16)
            k.dma("sp", cqa.t[:, :, :], cqa_d.ap().rearrange("(k p) t -> p k t", p=128), [R_cqa], [cqa.rg], "ld_cqa")
            k.dma("sp", ckv.t[:, :, :], ckv_d.ap().rearrange("(k p) t -> p k t", p=128), [R_ckv], [ckv.rg], "ld_ckv")
            k.dma("sp", kpe.t[:, :], kpe_d.ap(), [R_kpe], [kpe.rg], "ld_kpe")
            rmsnorm(ph, "nq", 6, lambda c, p_: (cqa.t[:, c, :], cqa.rg), lambda c: vcol(l, V_MQA + c), cqn, (6, 7))
            rmsnorm(ph, "nk", 4, lambda c, p_: (ckv.t[:, c, :], ckv.rg), lambda c: vcol(l, V_MKVA + c), ckn, (6, 7))
            k.act(sqk.t[:, :], kpe.t[:, :], AF.Square, [kpe.rg], [sqk.rg])
            wk = mk_wk(ph, "wkb", 6)
            obr = k.ring(ph, "obb", 4, [128, 512], F32)
            rst = k.ring(ph, "rstb", 2, [128, 512], F32)
            sqr = k.ring(ph, "sqbb", 4, [128, 512], BF16)

            def ent_q(h):
                rhs = lambda kk_, tt: (cqn.t[:, kk_, tt * 512:(tt + 1) * 512], cqn.rg)
                return [dict(W=W("w_uq"), Wrg=WR("w_uq"), c0=h * 192, w=64, kc=6, rhs=rhs),
                        dict(W=W("w_uq"), Wrg=WR("w_uq"), c0=h * 192 + 64, w=128, kc=6, rhs=rhs)]

            def head_rstd(sq_parts):
                n = len(sq_parts)
                for i, (ap_, rg_, rows) in enumerate(sq_parts):
                    k.op("pe", lambda en, i=i, ap_=ap_, rows=rows: en.matmul(ps[6][:, :], ones.t[0:rows, :], ap_, start=(i == 0), stop=(i == n - 1)),
                         [rg_, ones.rg], [psrg[6]] if i == 0 else [], [] if i == 0 else [psrg[6]])
                r_ = rst.next()
                rstd_from(r_, ps[6][:, :], [psrg[6]], 1.0 / 192)
                return r_

            def evac_q(h, tt, bl):
                br, bn = bl
                cs = slice(tt * 512, (tt + 1) * 512)
                s1 = sqr.next(); s2 = sqr.next()
                k.act(s1.t[0:64, :], ps[br][0:64, :], AF.Square, [psrg[br]], [s1.rg])
                k.act(s2.t[:, :], ps[bn][:, :], AF.Square, [psrg[bn]], [s2.rg])
                r_ = head_rstd([(s1.t[0:64, :], s1.rg, 64), (s2.t[:, :], s2.rg, 128)])
                ob = obr.next()
                k.stt("dve", ob.t[:, :], ps[bn][:, :], vcol(l, V_MQGN), r_.t[:, :], ALU.mult, ALU.mult, [psrg[bn], r_.rg, vec.rg], [ob.rg])
                store(ob, mqn_d.ap()[h * 128:(h + 1) * 128, cs], R_mqn)
                ob2 = norm_rope(wk, obr, br, tt, 64, vcol(l, V_MQGR, 64), 0, r_, swml, Cml, Sml, 7)
                store(ob2, mqr_d.ap()[h * 64:(h + 1) * 64, cs], R_mqr, rows=64)
            gemm(ph, "gq", ent_q, 8, evac_q, [0, 1, 2, 3, 4, 5])

            def ent_kv(h):
                return [dict(W=W("w_ukv"), Wrg=WR("w_ukv"), c0=h * 256, w=128, kc=4,
                             rhs=lambda kk_, tt: (ckn.t[:, kk_, tt * 512:(tt + 1) * 512], ckn.rg)),
                        dict(W=W("w_ukv"), Wrg=WR("w_ukv"), c0=h * 256 + 128, w=128, kc=4, tm=True,
                             lhs=lambda kk_, tb: (ckn.t[:, kk_, tb * 128:(tb + 1) * 128], ckn.rg))]

            def evac_kv(h, tt, bl):
                bk, bv = bl
                cs = slice(tt * 512, (tt + 1) * 512)
                s2 = sqr.next()
                k.act(s2.t[:, :], ps[bk][:, :], AF.Square, [psrg[bk]], [s2.rg])
                r_ = head_rstd([(sqk.t[0:64, cs], sqk.rg, 64), (s2.t[:, :], s2.rg, 128)])
                ob = obr.next()
                k.stt("dve", ob.t[:, :], ps[bk][:, :], vcol(l, V_MKGN), r_.t[:, :], ALU.mult, ALU.mult, [psrg[bk], r_.rg, vec.rg], [ob.rg])
                store(ob, kl_ap(1536 + h * 128, 128, cs), R_kpl[1])
                ob2 = norm_rope(wk, obr, None, tt, 64, vcol(l, V_MKGR, 64), 0, r_, swml, Cml, Sml, 7,
                                src_ap=kpe.t[0:64, cs], src_rgs=[kpe.rg])
                store(ob2, kl_ap(1024 + h * 64, 64, cs), R_kpl[1], rows=64)
                ob3 = obr.next()
                evict_copy(ob3.t[:, :], [ob3.rg], bv)
                for tb in range(4):
                    store(ob3, vl_ap(1, (tt * 4 + tb) * 128, h * 128), R_vpl[1], ap=ob3.t[:, tb * 128:(tb + 1) * 128])
            gemm(ph, "gkv", ent_kv, 8, evac_kv, [0, 1, 2, 3, 4, 5])
            k.barrier()
        for i in range(4, 10):
            k.cc(kpl[i], kpa[i], R_kpl[1], R_kpa[1], "cc_kml")
        for i in range(4):
            k.cc(vl[1][i], va[1][i], R_vpl[1], R_vpa[1], "cc_vml")
        with ExitStack() as ph:
            gw = k.sb(ph, "gw", [128, 8, 128], BF16)
            k.dma("pool", gw.t[:, :, :], lgw_in.ap()[l * 1024:(l + 1) * 1024, :].rearrange("(j c) d -> c j d", c=128),
                  [R0], [gw.rg], "ld_gw")
            ccol = k.sb(ph, "ccol", [128, 4], F32)
            k.act(ccol.t[:, :], vec.t[:, l * NV + V_LRUL:l * NV + V_LRUL + 4], AF.Exp, [vec.rg], [ccol.rg], scale=-1.0)
            k.ts("dve", ccol.t[:, :], ccol.t[:, :], 1.0, None, ALU.add, None, [ccol.rg], [ccol.rg])
            k.act(ccol.t[:, :], ccol.t[:, :], AF.Ln, [ccol.rg], [ccol.rg])
            k.ts("dve", ccol.t[:, :], ccol.t[:, :], -8.0, None, ALU.mult, None, [ccol.rg], [ccol.rg])
            up = k.sb(ph, "up", [128, S + 4], F32)
            uc = k.sb(ph, "uc", [128, S], F32)
            ucb = k.sb(ph, "ucb", [128, S], BF16)
            Rt = k.sb(ph, "Rt", [128, S], F32)
            It = k.sb(ph, "It", [128, S], F32)
            At = k.sb(ph, "At", [128, S], F32)
            Mt = k.sb(ph, "Mt", [128, S], F32)
            H0 = k.sb(ph, "H0", [128, S], F32)
            utmp = k.ring(ph, "utmp", 2, [128, T], F32)
            for cp in range(2):
                k.op("dve", lambda e: e.memset(up.t[:, 0:2], 0.0), [], [up.rg])
                k.op("dve", lambda e: e.memset(up.t[:, S + 2:S + 4], 0.0), [], [up.rg])
                for r in range(4):
                    dst = up.t[:, 2 + r * 1024:2 + (r + 1) * 1024]
                    for q in range(4):
                        t_ = utmp.next()
                        row0 = r * 256 + cp * 128
                        k.dma("sp", t_.t[:, :], ua[q].ap()[row0:row0 + 128, :], [R_ua], [t_.rg], f"ld_ut{(utmp.i - 1) % 2}")
                        if q == 0:
                            k.ts("dve", dst, t_.t[:, :], vcol(l, V_QM + q), None, ALU.mult, None, [t_.rg, vec.rg], [up.rg])
                        else:
                            k.stt("dve", dst, t_.t[:, :], vcol(l, V_QM + q), dst, ALU.mult, ALU.add, [t_.rg, vec.rg, up.rg], [up.rg])
                k.ts("dve", uc.t[:, :], up.t[:, 0:S], vcol(l, V_CONVW + cp * 4), vcol(l, V_CONVB + cp), ALU.mult, ALU.add,
                     [up.rg, vec.rg], [uc.rg])
                for j in range(1, 4):
                    k.stt("dve", uc.t[:, :], up.t[:, j:j + S], vcol(l, V_CONVW + cp * 4 + j), uc.t[:, :], ALU.mult, ALU.add,
                          [up.rg, vec.rg, uc.rg], [uc.rg])
                k.act(ucb.t[:, :], uc.t[:, :], AF.Copy, [uc.rg], [ucb.rg])
                for z in range(2):
                    for t8 in range(8):
                        cs = slice(t8 * 512, (t8 + 1) * 512)
                        for gi_, dstT in ((0, Rt), (1, It)):
                            b = (t8 * 2 + gi_) % 6
                            j = cp * 4 + z * 2 + gi_
                            k.op("pe", lambda en, b=b, j=j, cs=cs: en.matmul(ps[b][:, :], gw.t[:, j, :], ucb.t[:, cs], start=True, stop=True),
                                 [gw.rg, ucb.rg], [psrg[b]])
                            k.act(dstT.t[:, cs], ps[b][:, :], AF.Sigmoid, [psrg[b], vec.rg], [dstT.rg], bias=vcol(l, V_GATEB + j))
                    k.ts("dve", At.t[:, :], Rt.t[:, :], ccol.t[:, cp * 2 + z:cp * 2 + z + 1], None, ALU.mult, None, [Rt.rg, ccol.rg], [At.rg])
                    k.act(Mt.t[:, :], At.t[:, :], AF.Exp, [At.rg], [Mt.rg], scale=2.0)
                    k.act(At.t[:, :], At.t[:, :], AF.Exp, [At.rg], [At.rg])
                    k.ts("dve", Mt.t[:, :], Mt.t[:, :], -1.0, 1.0, ALU.mult, ALU.add, [Mt.rg], [Mt.rg])
                    k.ts("dve", Mt.t[:, :], Mt.t[:, :], 0.0, None, ALU.max, None, [Mt.rg], [Mt.rg])
                    k.act(Mt.t[:, :], Mt.t[:, :], AF.Sqrt, [Mt.rg], [Mt.rg])
                    k.tt("dve", Mt.t[:, :], Mt.t[:, :], It.t[:, :], ALU.mult, [Mt.rg, It.rg], [Mt.rg])
                    k.tt("dve", Mt.t[:, :], Mt.t[:, :], uc.t[:, :], ALU.mult, [Mt.rg, uc.rg], [Mt.rg])
                    if z == 0:
                        k.op("dve", lambda e: e.tensor_tensor_scan(out=H0.t[:, :], data0=At.t[:, :], data1=Mt.t[:, :], initial=0.0,
                                                                  op0=ALU.mult, op1=ALU.add), [At.rg, Mt.rg], [H0.rg])
                    else:
                        k.op("dve", lambda e: e.tensor_tensor_scan(out=Rt.t[:, ::-1], data0=At.t[:, ::-1], data1=Mt.t[:, ::-1], initial=0.0,
                                                                  op0=ALU.mult, op1=ALU.add), [At.rg, Mt.rg], [Rt.rg])
                k.tt("dve", H0.t[:, :], H0.t[:, :], Rt.t[:, :], ALU.add, [H0.rg, Rt.rg], [H0.rg])
                for j in range(2):
                    k.dma("sp", obl[cp * 2 + j].ap(), H0.t[j * 64:(j + 1) * 64, :], [H0.rg], [R_obl], "st_H0", part=True)
            k.barrier()
        for i in range(4):
            k.cc(obl[i], oba[i], R_obl, R_oba, "cc_ob")
        with ExitStack() as ph:
            lamw = k.sb(ph, "lamw", [128, 64], F32)
            lcol = k.sb(ph, "lcol", [128, 4], F32)
            base = l * NV + V_LAM
            for i in range(2):
                k.tt("dve", lamw.t[:, :], vec.t[:, base + i * 128:base + i * 128 + 64], vec.t[:, base + i * 128 + 64:base + i * 128 + 128],
                     ALU.mult, [vec.rg], [lamw.rg])
                k.op("dve", lambda e, i=i: e.reduce_sum(out=lcol.t[:, i:i + 1], in_=lamw.t[:, :], axis=AX.X), [lamw.rg], [lcol.rg])
            k.act(lcol.t[:, 0:2], lcol.t[:, 0:2], AF.Exp, [lcol.rg], [lcol.rg])
            k.tt("dve", lcol.t[:, 2:3], lcol.t[:, 1:2], lcol.t[:, 0:1], ALU.subtract, [lcol.rg], [lcol.rg])
            k.tt("dve", lcol.t[:, 2:3], lcol.t[:, 2:3], vcol(l, V_LAMC), ALU.subtract, [lcol.rg, vec.rg], [lcol.rg])
            k.tt("dve", lcol.t[:, 3:4], vcol(l, V_DASUB), vcol(l, V_LAMC + 1), ALU.mult, [vec.rg], [lcol.rg])
            NB = 2
            kT = k.ring(ph, "kT", NB, [128, S], BF16)
            kR = k.ring(ph, "kR", NB, [64, S], BF16)
            vT = k.ring(ph, "vT", NB, [128, 32, 128], BF16)
            qT = k.ring(ph, "qT", NB, [128, T], BF16)
            qR = k.ring(ph, "qR", NB, [64, T], BF16)
            ptr = k.ring(ph, "pt", 4, [128, 512], BF16)
            wk = k.ring(ph, "wka", 4, [128, 512], F32)
            obr = k.ring(ph, "oba", 3, [128, 512], F32)
            sqr = k.ring(ph, "sqa", 2, [128, 512], BF16)
            def load_v(vt_, a, h, slot):
                for r in range(4):
                    for i in range(4):
                        kb0 = r * 8 + i * 2
                        k.dma("pool", vt_.t[:, kb0:kb0 + 2, :],
                              va[a][i].ap()[r * 256:(r + 1) * 256, h * 128:(h + 1) * 128].rearrange("(j p) c -> p j c", p=128),
                              [R_vpa[a]], [vt_.rg], f"ld_vT{slot}", part=True)
            for hh in range(16):
                mla = hh >= 8
                h = hh % 8
                kt_ = kT.next(); vt_ = vT.next(); qt_ = qT.next()
                slot = (kT.i - 1) % NB
                if not mla:
                    for r in range(4):
                        k.dma("pool", kt_.t[:, r * 1024:(r + 1) * 1024], ka_ap(r, h * 128, 128),
                              [R_kpa[0]], [kt_.rg], f"ld_kT{slot}", part=True)
                    k.dma("pool", qt_.t[:, :], qda_d.ap()[h * 128:(h + 1) * 128, :], [R_qda], [qt_.rg], f"ld_qT{slot}")
                    load_v(vt_, 0, h, slot)
                    maps = [[(kt_.t[0:64, :], qt_.t[0:64, :], [kt_.rg, qt_.rg])], [(kt_.t[64:128, :], qt_.t[64:128, :], [kt_.rg, qt_.rg])]]
                    scale = 64 ** -0.5
                else:
                    kr_ = kR.next(); qr_ = qR.next()
                    for r in range(4):
                        k.dma("pool", kt_.t[:, r * 1024:(r + 1) * 1024], ka_ap(r, 1536 + h * 128, 128), [R_kpa[1]], [kt_.rg], f"ld_kT{slot}", part=True)
                        k.dma("pool", kr_.t[:, r * 1024:(r + 1) * 1024], ka_ap(r, 1024 + h * 64, 64), [R_kpa[1]], [kr_.rg], f"ld_kR{slot}", part=True)
                    k.dma("pool", qt_.t[:, :], mqn_d.ap()[h * 128:(h + 1) * 128, :], [R_mqn], [qt_.rg], f"ld_qT{slot}")
                    k.dma("pool", qr_.t[:, :], mqr_d.ap()[h * 64:(h + 1) * 64, :], [R_mqr], [qr_.rg], f"ld_qR{slot}")
                    load_v(vt_, 1, h, slot)
                    maps = [[(kt_.t[:, :], qt_.t[:, :], [kt_.rg, qt_.rg]), (kr_.t[:, :], qr_.t[:, :], [kr_.rg, qr_.rg])]]
                    scale = 192 ** -0.5
                nm = len(maps)
                for qb in range(2):
                    qs = slice(qb * 512, (qb + 1) * 512)

                    def emit_S(kb):
                        for m, parts in enumerate(maps):
                            b = m * 2 + (kb % 2)
                            for pi, (ka, qa, rgs) in enumerate(parts):
                                first = pi == 0
                                k.op("pe", lambda en, b=b, ka=ka, qa=qa, pi=pi, parts=parts: en.matmul(
                                    ps[b][:, :], ka[:, kb * 128:(kb + 1) * 128], qa[:, qs], start=(pi == 0), stop=(pi == len(parts) - 1)),
                                    rgs, [psrg[b]] if first else [], [] if first else [psrg[b]])
                    emit_S(0)
                    for kb in range(32):
                        if kb + 1 < 32:
                            emit_S(kb + 1)
                        for m in range(nm):
                            b = m * 2 + (kb % 2)
                            p_ = ptr.next()
                            k.act(p_.t[:, :], ps[b][:, :], AF.Exp, [psrg[b]], [p_.rg], scale=scale)
                            first = kb == 0
                            k.op("pe", lambda en, m=m, p_=p_: en.matmul(ps[4 + m][:, :], vt_.t[:, kb, :], p_.t[:, :], start=(kb == 0), stop=(kb == 31)),
                                 [vt_.rg, p_.rg], [psrg[4 + m]] if first else [], [] if first else [psrg[4 + m]])
                            k.op("pe", lambda en, m=m, p_=p_: en.matmul(ps[6 + m][:, :], ones.t[:, :], p_.t[:, :], start=(kb == 0), stop=(kb == 31)),
                                 [ones.rg, p_.rg], [psrg[6 + m]] if first else [], [] if first else [psrg[6 + m]])
                    if not mla:
                        r1 = wk.next(); r2 = wk.next(); o_ = wk.next()
                        k.op("dve", lambda e: e.reciprocal(out=r1.t[:, :], in_=ps[6][:, :]), [psrg[6]], [r1.rg])
                        k.op("dve", lambda e: e.reciprocal(out=r2.t[:, :], in_=ps[7][:, :]), [psrg[7]], [r2.rg])
                        k.tt("dve", r1.t[:, :], ps[4][:, :], r1.t[:, :], ALU.mult, [psrg[4], r1.rg], [r1.rg])
                        k.tt("dve", r2.t[:, :], ps[5][:, :], r2.t[:, :], ALU.mult, [psrg[5], r2.rg], [r2.rg])
                        k.stt("dve", o_.t[:, :], r2.t[:, :], lcol.t[:, 2:3], r1.t[:, :], ALU.mult, ALU.add, [r1.rg, r2.rg, lcol.rg], [o_.rg])
                        s_ = sqr.next()
                        k.act(s_.t[:, :], o_.t[:, :], AF.Square, [o_.rg], [s_.rg])
                        k.op("pe", lambda en: en.matmul(ps[0][:, :], ones.t[:, :], s_.t[:, :], start=True, stop=True), [ones.rg, s_.rg], [psrg[0]])
                        rstd_from(r1, ps[0][:, :], [psrg[0]], 1.0 / 128)
                        ob = obr.next()
                        k.stt("dve", ob.t[:, :], o_.t[:, :], lcol.t[:, 3:4], r1.t[:, :], ALU.mult, ALU.mult, [o_.rg, r1.rg, lcol.rg], [ob.rg])
                        store(ob, oa_d.ap()[h * 128:(h + 1) * 128, qs], R_oa)
                    else:
                        r1 = wk.next()
                        k.op("dve", lambda e: e.reciprocal(out=r1.t[:, :], in_=ps[6][:, :]), [psrg[6]], [r1.rg])
                        ob = obr.next()
                        k.tt("dve", ob.t[:, :], ps[4][:, :], r1.t[:, :], ALU.mult, [psrg[4], r1.rg], [ob.rg])
                        store(ob, oc_d.ap()[h * 128:(h + 1) * 128, qs], R_oc)
            k.barrier()
        with ExitStack() as ph_o, ExitStack() as ph:
            zT = k.sb(ph_o, "zT", [128, 32, T], BF16)
            oT = [k.sb(ph, f"oT{n}", [128, 8, T], BF16) for n in range(3)]
            glr = k.sb(ph, "glr", [128, 2, T], BF16)
            k.dma("pool", oT[0].t[:, :, :], oa_d.ap().rearrange("(k p) t -> p k t", p=128), [R_oa], [oT[0].rg], "ld_oa")
            k.dma("pool", oT[2].t[:, :, :], oc_d.ap().rearrange("(k p) t -> p k t", p=128), [R_oc], [oT[2].rg], "ld_oc")
            k.dma("pool", glr.t[:, :, :], glr_d.ap().rearrange("(k p) t -> p k t", p=128), [R_glr], [glr.rg], "ld_glr")
            obt = k.ring(ph, "obt", 2, [128, T], F32)
            acc = k.sb(ph, "oacc", [128, T], F32)
            gl = k.sb(ph, "gelc", [128, T], F32)
            for c in range(8):
                for q in range(4):
                    t_ = obt.next()
                    for j in range(2):
                        k.dma("sp", t_.t[j * 64:(j + 1) * 64, :], oba[(c % 2) * 2 + j].ap()[(c // 2) * 64:(c // 2 + 1) * 64, q * 1024:(q + 1) * 1024],
                              [R_oba], [t_.rg], f"ld_obt{(obt.i - 1) % 2}", part=True)
                    if q == 0:
                        k.ts("dve", acc.t[:, :], t_.t[:, :], vcol(l, V_QM), None, ALU.mult, None, [t_.rg, vec.rg], [acc.rg])
                    else:
                        k.stt("dve", acc.t[:, :], t_.t[:, :], vcol(l, V_QM + q), acc.t[:, :], ALU.mult, ALU.add, [t_.rg, vec.rg, acc.rg], [acc.rg])
                k.dma("sp", gl.t[:, :], gel_d.ap()[c * 128:(c + 1) * 128, :], [R_gel], [gl.rg], "ld_gel")
                k.tt("dve", oT[1].t[:, c, :], acc.t[:, :], gl.t[:, :], ALU.mult, [acc.rg, gl.rg], [oT[1].rg])
            G = k.ring(ph, "G", 2, [128, 512], F32)
            Z = [k.sb(ph, f"Z{i}", [128, 512], F32) for i in range(2)]
            tmp = k.ring(ph, "ztmp", 2, [128, 512], F32)

            def ent_a(gi):
                dc, n = gi // 3, gi % 3
                return [dict(W=W("w_gu"), Wrg=WR("w_gu"), c0=n * D + dc * 128, w=128, kc=2,
                             rhs=lambda kk_, tt: (glr.t[:, kk_, tt * 512:(tt + 1) * 512], glr.rg)),
                        dict(W=W("w_bo"), Wrg=WR("w_bo"), c0=dc * 128, w=128, kc=8, krow0=n * 8,
                             rhs=lambda kk_, tt, n=n: (oT[n].t[:, kk_, tt * 512:(tt + 1) * 512], oT[n].rg))]

            def evac_a(gi, tt, bl):
                dc, n = gi // 3, gi % 3
                bg, by = bl
                g_ = G.next()
                k.act(g_.t[:, :], ps[bg][:, :], AF.Sigmoid, [psrg[bg], vec.rg], [g_.rg], bias=vcol(l, V_BGATE + n * 32 + dc))
                if n == 0:
                    k.tt("dve", Z[tt].t[:, :], g_.t[:, :], ps[by][:, :], ALU.mult, [g_.rg, psrg[by]], [Z[tt].rg])
                else:
                    t_ = tmp.next()
                    k.tt("dve", t_.t[:, :], g_.t[:, :], ps[by][:, :], ALU.mult, [g_.rg, psrg[by]], [t_.rg])
                    if n == 1:
                        k.tt("pool", Z[tt].t[:, :], Z[tt].t[:, :], t_.t[:, :], ALU.add, [Z[tt].rg, t_.rg], [Z[tt].rg])
                    else:
                        k.tt("pool", zT.t[:, dc, tt * 512:(tt + 1) * 512], Z[tt].t[:, :], t_.t[:, :], ALU.add, [Z[tt].rg, t_.rg], [zT.rg])
            gemm(ph, "ga", ent_a, 96, evac_a, [0, 1, 2, 3, 4, 5, 6, 7])
            k.barrier()
            ph.close()
            xr = k.ring(ph, "xr3", 3, [128, 512], F32)
            obr = k.ring(ph, "ob3", 3, [128, 512], F32)

            def ent_o(dc):
                return [dict(W=W("w_out"), Wrg=WR("w_out"), c0=dc * 128, w=128, kc=32,
                             rhs=lambda kk_, tt: (zT.t[:, kk_, tt * 512:(tt + 1) * 512], zT.rg))]

            def evac_o(dc, tt, bl):
                b = bl[0]
                cs = slice(tt * 512, (tt + 1) * 512)
                x_ = xr.next()
                k.dma("sp", x_.t[:, :], xsrc_h.ap()[dc * 128:(dc + 1) * 128, cs], [xsrc_rg], [x_.rg], f"ld_x3{(xr.i - 1) % 3}")
                ob = obr.next()
                k.tt("dve", ob.t[:, :], ps[b][:, :], x_.t[:, :], ALU.add, [psrg[b], x_.rg], [ob.rg])
                store(ob, out_d.ap()[dc * 128:(dc + 1) * 128, cs], R_out)
            gemm(ph, "go", ent_o, 32, evac_o, [0, 1, 2, 3, 4, 5, 6, 7])
            k.barrier()
        with ExitStack() as ph:
            xn2 = k.sb(ph, "xn2", [128, 32, T], BF16)
            rmsnorm(ph, "n2", 32, dram_chunk_loader(ph, "n2", out_d, R_out), lambda c: vcol(l, V_G2 + c), xn2, (6, 7))
            sl = k.ring(ph, "silu", 2, [128, 512], F32)
            obr = k.ring(ph, "obf", 3, [128, 512], F32)
            rhs = lambda kk_, tt: (xn2.t[:, kk_, tt * 512:(tt + 1) * 512], xn2.rg)

            def ent_f(hc):
                return [dict(W=W("w_fg"), Wrg=WR("w_fg"), c0=hc * 128, w=128, kc=32, rhs=rhs),
                        dict(W=W("w_fu"), Wrg=WR("w_fu"), c0=hc * 128, w=128, kc=32, rhs=rhs)]

            def evac_f(hc, tt, bl):
                bg, bu = bl
                s_ = sl.next()
                k.act(s_.t[:, :], ps[bg][:, :], AF.Silu, [psrg[bg]], [s_.rg])
                ob = obr.next()
                k.tt("dve", ob.t[:, :], s_.t[:, :], ps[bu][:, :], ALU.mult, [s_.rg, psrg[bu]], [ob.rg])
                store(ob, hT_d.ap()[hc * 128:(hc + 1) * 128, tt * 512:(tt + 1) * 512], R_hT)
            gemm(ph, "gf", ent_f, FH // 128, evac_f, [0, 1, 2, 3, 4, 5])
            k.barrier()
        for th in range(2):
            with ExitStack() as ph:
                hT = k.sb(ph, "hT", [128, 86, 512], BF16)
                hv = hT_d.ap().rearrange("(k p) t -> p k t", p=128)
                for k0 in range(0, 86, 16):
                    k1 = min(86, k0 + 16)
                    k.dma("pool", hT.t[:, k0:k1, :], hv[:, k0:k1, th * 512:(th + 1) * 512], [R_hT], [hT.rg], "ld_hT", part=True)
                xr = k.ring(ph, "xr4", 3, [128, 512], F32)
                obr = k.ring(ph, "ob4", 3, [128, 512], F32)

                def ent_d(dc):
                    return [dict(W=W("w_fd"), Wrg=WR("w_fd"), c0=dc * 128, w=128, kc=86,
                                 rhs=lambda kk_, tt: (hT.t[:, kk_, :], hT.rg))]

                def evac_d(dc, tt, bl):
                    b = bl[0]
                    cs = slice(th * 512, (th + 1) * 512)
                    x_ = xr.next()
                    k.dma("sp", x_.t[:, :], out_d.ap()[dc * 128:(dc + 1) * 128, cs], [R_out], [x_.rg], f"ld_x4{(xr.i - 1) % 3}")
                    ob = obr.next()
                    k.tt("dve", ob.t[:, :], ps[b][:, :], x_.t[:, :], ALU.add, [psrg[b], x_.rg], [ob.rg])
                    store(ob, out_d.ap()[dc * 128:(dc + 1) * 128, cs], R_out)
                gemm(ph, "gd", ent_d, 32, evac_d, [0, 1, 2, 3, 4, 5], NS=3, ntt=1)
                k.barrier()
    k.barrier()
    print("n_sems", len(k.sem), "instr counts", {e: k.cnt[e] for e in k.eng})
    return nc, es


def _patch_rows():
    pass


_CACHE = {}


def _consts():
    c = np.zeros((128, NCST), np.float32)
    for m in range(128):
        c[(m // 64) * 64:(m // 64) * 64 + 64, C_GSUM + m] = 1.0
        r = m % 64
        p = m + 8 if r < 8 else (m - 8 if r < 16 else m)
        c[p, C_SWDA + m] = 1.0
    for m in range(64):
        p = m + 32 if m < 32 else m - 32
        c[p, C_SWMLA + m] = 1.0
    theta = np.float32(500000.0)
    inv_da = (theta ** (-np.arange(0, 16, 2, dtype=np.float32) / np.float32(16))).astype(np.float32)
    inv_ml = (theta ** (-np.arange(0, 64, 2, dtype=np.float32) / np.float32(64))).astype(np.float32)
    for m in range(128):
        r = m % 64
        if r < 16:
            c[m, C_INVDA] = inv_da[r % 8]
            c[m, C_SGNDA] = -1.0 if r < 8 else 1.0
    for m in range(64):
        c[m, C_INVMLA] = inv_ml[m % 32]
        c[m, C_SGNMLA] = -1.0 if m < 32 else 1.0
    return c


def _run(inputs, depth):
    f = lambda a: np.ascontiguousarray(np.asarray(a, dtype=np.float32))
    x = np.asarray(inputs["x"], dtype=np.float32)
    pos = np.asarray(inputs["positions"]).astype(np.int32)
    if depth not in _CACHE:
        _CACHE[depth] = build(depth)
    nc, _es = _CACHE[depth]
    cst = _consts()
    col = lambda v, n: np.asarray(v, np.float32).reshape(n, 128).T
    wmap = {"w_in": inputs["w_in"], "w_uq": inputs["mla_w_uq"], "w_ukv": inputs["mla_w_ukv"], "w_gu": inputs["w_gate_up"],
            "w_bo": np.asarray(inputs["w_branch_out"]).reshape(NL, 3072, D), "w_out": inputs["w_out"],
            "w_fg": inputs["w_ffn_gate"], "w_fu": inputs["w_ffn_up"], "w_fd": inputs["w_ffn_down"]}
    wfl = {nm: np.ascontiguousarray(np.asarray(wmap[nm][:depth], np.float32).reshape(depth * kk, nn)) for (nm, kk, nn) in WNAMES}
    in_maps = []
    for c in range(8):
        b, q = c // 4, c % 4
        m = {}
        m["xT_in"] = np.ascontiguousarray(x[b, q * T:(q + 1) * T, :].T)
        m["pos_in"] = np.ascontiguousarray(np.broadcast_to(pos[b, q * T:(q + 1) * T][None, :], (128, T)))
        vec = np.zeros((128, depth, NV), np.float32)
        lgw = np.zeros((depth, 2, 2, 2, 128, 128), np.float32)
        for l in range(depth):
            v = vec[:, l, :]
            v[:, V_G1:V_G1 + 32] = col(inputs["mixer_norm_g"][l], 32)
            v[:, V_G2:V_G2 + 32] = col(inputs["ffn_norm_g"][l], 32)
            v[:, V_DAQ] = np.tile(np.asarray(inputs["da_q_norm_g"][l], np.float32), 2)
            v[:, V_DAK] = np.tile(np.asarray(inputs["da_k_norm_g"][l], np.float32), 2)
            v[:, V_DASUB] = np.asarray(inputs["da_subln_g"][l], np.float32)
            li = 0.8 - 0.6 * math.exp(-0.3 * l)
            v[:, V_LAMC] = li
            v[:, V_LAMC + 1] = 1.0 - li
            v[:, V_LAM:V_LAM + 256] = np.asarray(inputs["da_lambda"][l], np.float32).reshape(1, 256)
            for cp in range(2):
                ch = slice(q * 256 + cp * 128, q * 256 + (cp + 1) * 128)
                for j in range(4):
                    v[:, V_CONVW + cp * 4 + j] = np.asarray(inputs["lru_conv_w"][l][j, ch], np.float32)
                v[:, V_CONVB + cp] = np.asarray(inputs["lru_conv_b"][l][ch], np.float32)
                for z in range(2):
                    v[:, V_LRUL + cp * 2 + z] = np.asarray(inputs["lru_L"][l][z, ch], np.float32)
                    for g_ in range(2):
                        v[:, V_GATEB + cp * 4 + z * 2 + g_] = np.asarray(inputs["lru_gate_b"][l][z, g_, ch], np.float32)
                        lgw[l, cp, z, g_] = np.asarray(inputs["lru_gate_w"][l][z, g_, q * 2 + cp], np.float32)
            v[:, V_MQA:V_MQA + 6] = col(inputs["mla_q_a_norm_g"][l], 6)
            v[:, V_MKVA:V_MKVA + 4] = col(inputs["mla_kv_a_norm_g"][l], 4)
            qg = np.asarray(inputs["mla_q_norm_g"][l], np.float32); kg = np.asarray(inputs["mla_k_norm_g"][l], np.float32)
            v[0:64, V_MQGR] = qg[0:64]; v[:, V_MQGN] = qg[64:192]
            v[0:64, V_MKGR] = kg[0:64]; v[:, V_MKGN] = kg[64:192]
            bg = np.asarray(inputs["b_gate"][l], np.float32)
            for n in range(3):
                v[:, V_BGATE + n * 32:V_BGATE + (n + 1) * 32] = col(bg[n], 32)
            v[:, V_QM + q] = 1.0
        m["vec_in"] = np.ascontiguousarray(vec.reshape(128, depth * NV))
        m["cst_in"] = cst
        m["lgw_in"] = np.ascontiguousarray(lgw.reshape(depth * 8 * 128, 128))
        for (nm, kk, nn) in WNAMES:
            m[nm + "_w"] = wfl[nm]
        in_maps.append(m)
    res = run_bass_kernel_spmd(nc, in_maps, core_ids=list(range(8)))
    out = np.zeros((2, S, D), np.float32)
    for c in range(8):
        b, q = c // 4, c % 4
        out[b, q * T:(q + 1) * T, :] = res.results[c]["out"].T
    return out


def kernel(**inputs):
    return _run(inputs, NL)
```
